# Optimizing a Trainium2 kernel written in Bass

```python
import math
import jax, jax.numpy as jnp
from jax import lax
import numpy as np

D_MODEL = 1024
BATCH = 4
SEQ = 4096
DEPTH = 2
DEC_BATCH = 128
DEC_SEQ = 1
PAST_LEN = 2048
PAGE_SIZE = 128

N_MIXERS = 2
N_ATTN_LAYERS = (DEPTH + N_MIXERS - 1) // N_MIXERS
N_CONV_LAYERS = DEPTH // N_MIXERS
DIL_GROUPS = ((128, 1), (512, 4), (2048, 16))
N_GROUPS = len(DIL_GROUPS)
HEADS_PER_GROUP = 8
HEAD_DIM = 128
ATTN_WIDTH = HEADS_PER_GROUP * HEAD_DIM
N_BUCKETS = 32
MAX_DISTANCE = 2048
CONV_WIDTH = 31
FFN_HIDDEN = -(-8 * D_MODEL // (3 * 256)) * 256
EPS = 1e-6
NEG_INF = -1e30

kernel_name = 'hybrid_dilated_attn_conformer_decoder_step'


def rms_norm(x, g):
    xf = x.astype(jnp.float32)
    y = xf * lax.rsqrt(jnp.mean(xf * xf, axis=-1, keepdims=True) + EPS)
    return y.astype(x.dtype) * g


def layer_norm(x, g, b):
    xf = x.astype(jnp.float32)
    mu = jnp.mean(xf, axis=-1, keepdims=True)
    var = jnp.mean(jnp.square(xf - mu), axis=-1, keepdims=True)
    return ((xf - mu) * lax.rsqrt(var + EPS)).astype(x.dtype) * g + b


def t5_bucket(dist):
    max_exact = N_BUCKETS // 2
    n = jnp.maximum(dist, 1).astype(jnp.float32)
    large = max_exact + (jnp.log(n / max_exact) / math.log(MAX_DISTANCE / max_exact)
                         * (N_BUCKETS - max_exact)).astype(jnp.int32)
    large = jnp.minimum(large, N_BUCKETS - 1)
    return jnp.where(dist < max_exact, dist, large)


def dilated_attn_prompt(q, k, v, bias_g, window, dilation):
    batch, seq, heads, dh = q.shape
    span = window // dilation
    unit = span * dilation
    padded = -(-seq // unit) * unit
    nb = padded // unit

    def to_blocks(a):
        a = jnp.pad(a, ((0, 0), (0, padded - seq), (0, 0), (0, 0)))
        return a.reshape(batch, nb, span, dilation, heads, dh)

    def with_prev(a):
        prev = jnp.pad(a[:, :-1], ((0, 0), (1, 0), (0, 0), (0, 0), (0, 0), (0, 0)))
        return jnp.concatenate([prev, a], axis=2)

    qb = to_blocks(q)
    kband = with_prev(to_blocks(k))
    vband = with_prev(to_blocks(v))
    logits = jnp.einsum('bnqrhd,bnkrhd->bnqrhk', qb, kband).astype(jnp.float32) * (HEAD_DIM ** -0.5)
    qi = jnp.arange(span)[:, None]
    ki = jnp.arange(2 * span)[None, :]
    delta = qi + span - ki
    in_band = (delta >= 0) & (delta <= span)
    bucket = t5_bucket(jnp.clip(delta, 0, span) * dilation)
    bias = jnp.transpose(bias_g[bucket], (0, 2, 1)).astype(jnp.float32)
    not_before_start = (jnp.arange(nb)[:, None, None] > 0) | (ki >= span)[None]
    valid = in_band[None] & not_before_start
    logits = logits + bias[None, None, :, None]
    logits = jnp.where(valid[None, :, :, None, None, :], logits, NEG_INF)
    lse = jax.nn.logsumexp(logits, axis=-1)
    p = jnp.exp(logits - lse[..., None]).astype(v.dtype)
    o = jnp.einsum('bnqrhk,bnkrhd->bnqrhd', p, vband)
    return o.reshape(batch, padded, heads, dh)[:, :seq], lse.reshape(batch, padded, heads)[:, :seq]


def dilated_attn_sample(q, k_all, v_all, bias_g, window, dilation, n_buf):
    n_new = q.shape[1]
    span = window // dilation
    j = jnp.arange(span + 1)
    idx = n_buf + jnp.arange(n_new)[:, None] - j[None, :] * dilation
    valid = idx >= 0
    idx = jnp.maximum(idx, 0)
    kg = k_all[:, idx]
    vg = v_all[:, idx]
    logits = jnp.einsum('bthd,btjhd->bthj', q, kg).astype(jnp.float32) * (HEAD_DIM ** -0.5)
    bias = jnp.transpose(bias_g[t5_bucket(j * dilation)], (1, 0)).astype(jnp.float32)
    logits = logits + bias[None, None]
    logits = jnp.where(valid[None, :, None, :], logits, NEG_INF)
    lse = jax.nn.logsumexp(logits, axis=-1)
    p = jnp.exp(logits - lse[..., None]).astype(v_all.dtype)
    o = jnp.einsum('bthj,btjhd->bthd', p, vg)
    return o, lse


def dilated_mixture(h, w_qkv, w_o, rel_bias, bufs):
    batch, n_tok, _ = h.shape
    qkv = (h @ w_qkv).reshape(batch, n_tok, N_GROUPS, 3, HEADS_PER_GROUP, HEAD_DIM)
    outs, lses, new_state = [], [], []
    for g, (window, dilation) in enumerate(DIL_GROUPS):
        q, k, v = qkv[:, :, g, 0], qkv[:, :, g, 1], qkv[:, :, g, 2]
        bias_g = rel_bias[:, g * HEADS_PER_GROUP:(g + 1) * HEADS_PER_GROUP]
        if bufs is None:
            o, lse = dilated_attn_prompt(q, k, v, bias_g, window, dilation)
            keep = min(window, n_tok)
            new_state.append(jnp.stack([k[:, n_tok - keep:], v[:, n_tok - keep:]], axis=1))
        else:
            buf = bufs[g]
            n_buf = buf.shape[2]
            k_all = jnp.concatenate([buf[:, 0], k], axis=1)
            v_all = jnp.concatenate([buf[:, 1], v], axis=1)
            o, lse = dilated_attn_sample(q, k_all, v_all, bias_g, window, dilation, n_buf)
            new_state.append(jnp.stack([k, v], axis=1))
        outs.append(o)
        lses.append(lse)
    w = jax.nn.softmax(jnp.stack(lses, axis=2), axis=2).astype(h.dtype)
    o = jnp.einsum('btghd,btgh->bthd', jnp.stack(outs, axis=2), w)
    return o.reshape(batch, n_tok, ATTN_WIDTH) @ w_o, new_state


def conformer_conv(h, w_pw1, b_pw1, w_dw, b_dw, ln_g, ln_b, w_pw2, b_pw2, buf):
    batch, n_tok, d = h.shape
    a, gate = jnp.split(h @ w_pw1 + b_pw1, 2, axis=-1)
    u = a * jax.nn.sigmoid(gate)
    prefix = jnp.zeros((batch, CONV_WIDTH - 1, d), u.dtype) if buf is None else buf.astype(u.dtype)
    ext = jnp.concatenate([prefix, u], axis=1)
    z = lax.conv_general_dilated(ext, w_dw[:, None, :].astype(u.dtype), window_strides=(1,), padding='VALID',
                                 dimension_numbers=('NWC', 'WIO', 'NWC'), feature_group_count=d) + b_dw
    z = jax.nn.silu(layer_norm(z, ln_g, ln_b))
    return z @ w_pw2 + b_pw2, ext[:, -(CONV_WIDTH - 1):]


def swiglu(h, w_gate, w_up, w_down):
    return (jax.nn.silu(h @ w_gate) * (h @ w_up)) @ w_down


def trunk(x, c, attn_bufs, conv_bufs, w_mod, b_mod, g_mix, g_ffn, g_final, w_qkv, w_o, rel_bias,
          w_pw1, b_pw1, w_dw, b_dw, ln_g, ln_b, w_pw2, b_pw2, w_gate, w_up, w_down):
    new_attn = [[] for _ in range(N_GROUPS)]
    new_conv = []
    for i in range(DEPTH):
        mod = jax.nn.silu(c) @ w_mod[i] + b_mod[i]
        sh1, sc1, g1, sh2, sc2, g2 = jnp.split(mod[:, None, :], 6, axis=-1)
        h = rms_norm(x, g_mix[i]) * (1 + sc1) + sh1
        if i % N_MIXERS == 0:
            a = i // N_MIXERS
            bufs = None if attn_bufs is None else tuple(b[a] for b in attn_bufs)
            out, st = dilated_mixture(h, w_qkv[a], w_o[a], rel_bias, bufs)
            for g in range(N_GROUPS):
                new_attn[g].append(st[g])
        else:
            b = i // N_MIXERS
            buf = None if conv_bufs is None else conv_bufs[b]
            out, st = conformer_conv(h, w_pw1[b], b_pw1[b], w_dw[b], b_dw[b], ln_g[b], ln_b[b],
                                     w_pw2[b], b_pw2[b], buf)
            new_conv.append(st)
        x = x + g1 * out
        h = rms_norm(x, g_ffn[i]) * (1 + sc2) + sh2
        x = x + g2 * swiglu(h, w_gate[i], w_up[i], w_down[i])
    y = rms_norm(x, g_final)
    return y, [jnp.stack(s, axis=0) for s in new_attn], jnp.stack(new_conv, axis=0)


def setup_inputs(seed: int = 0) -> dict:
    key = jax.random.key(seed)
    ks = jax.random.split(key, 32)
    d = D_MODEL
    n_a, n_b = N_ATTN_LAYERS, N_CONV_LAYERS
    buf_len = [min(w, PAST_LEN) for w, _ in DIL_GROUPS]

    def nrm(k, shape, scale=1.0):
        return jax.random.normal(k, shape, jnp.float32) * scale

    return {
        'x_prompt': nrm(ks[0], (BATCH, SEQ, d)),
        'x_sample': nrm(ks[1], (DEC_BATCH, DEC_SEQ, d)),
        'cache_kv_w128': nrm(ks[2], (n_a, DEC_BATCH, 2, buf_len[0], HEADS_PER_GROUP, HEAD_DIM)),
        'cache_kv_w512': nrm(ks[3], (n_a, DEC_BATCH, 2, buf_len[1], HEADS_PER_GROUP, HEAD_DIM)),
        'cache_kv_w2048': nrm(ks[4], (n_a, DEC_BATCH, 2, buf_len[2], HEADS_PER_GROUP, HEAD_DIM)),
        'state_conv': nrm(ks[5], (n_b, DEC_BATCH, CONV_WIDTH - 1, d), 0.5),
        'c_prompt': nrm(ks[6], (BATCH, d)),
        'c_sample': nrm(ks[7], (DEC_BATCH, d)),
        'w_mod': nrm(ks[8], (DEPTH, d, 6 * d), 0.5 * d ** -0.5),
        'b_mod': nrm(ks[9], (DEPTH, 6 * d), 0.02),
        'g_mix': 1.0 + nrm(ks[10], (DEPTH, d), 0.02),
        'g_ffn': 1.0 + nrm(ks[11], (DEPTH, d), 0.02),
        'g_final': 1.0 + nrm(ks[12], (d,), 0.02),
        'w_qkv': nrm(ks[13], (n_a, d, N_GROUPS * 3 * ATTN_WIDTH), d ** -0.5),
        'w_o': nrm(ks[14], (n_a, ATTN_WIDTH, d), ATTN_WIDTH ** -0.5),
        'rel_bias': nrm(ks[15], (N_BUCKETS, N_GROUPS * HEADS_PER_GROUP), 0.2),
        'w_pw1': nrm(ks[16], (n_b, d, 2 * d), d ** -0.5),
        'b_pw1': nrm(ks[17], (n_b, 2 * d), 0.02),
        'w_dw': nrm(ks[18], (n_b, CONV_WIDTH, d), CONV_WIDTH ** -0.5),
        'b_dw': nrm(ks[19], (n_b, d), 0.02),
        'ln_g': 1.0 + nrm(ks[20], (n_b, d), 0.02),
        'ln_b': nrm(ks[21], (n_b, d), 0.02),
        'w_pw2': nrm(ks[22], (n_b, d, d), d ** -0.5),
        'b_pw2': nrm(ks[23], (n_b, d), 0.02),
        'w_gate': nrm(ks[24], (DEPTH, d, FFN_HIDDEN), d ** -0.5),
        'w_up': nrm(ks[25], (DEPTH, d, FFN_HIDDEN), d ** -0.5),
        'w_down': nrm(ks[26], (DEPTH, FFN_HIDDEN, d), FFN_HIDDEN ** -0.5),
    }


def reference(x_prompt, x_sample, cache_kv_w128, cache_kv_w512, cache_kv_w2048, state_conv,
              c_prompt, c_sample, w_mod, b_mod, g_mix, g_ffn, g_final, w_qkv, w_o, rel_bias,
              w_pw1, b_pw1, w_dw, b_dw, ln_g, ln_b, w_pw2, b_pw2, w_gate, w_up, w_down):
    y_prompt, attn_p, conv_p = trunk(x_prompt, c_prompt, None, None, w_mod, b_mod, g_mix, g_ffn, g_final,
                                     w_qkv, w_o, rel_bias, w_pw1, b_pw1, w_dw, b_dw, ln_g, ln_b,
                                     w_pw2, b_pw2, w_gate, w_up, w_down)
    y_sample, attn_s, conv_s = trunk(x_sample, c_sample, (cache_kv_w128, cache_kv_w512, cache_kv_w2048),
                                     state_conv, w_mod, b_mod, g_mix, g_ffn, g_final,
                                     w_qkv, w_o, rel_bias, w_pw1, b_pw1, w_dw, b_dw, ln_g, ln_b,
                                     w_pw2, b_pw2, w_gate, w_up, w_down)
    kv128_p, kv512_p, kv2048_p = attn_p
    kv128_s, kv512_s, kv2048_s = attn_s
    return (y_prompt, y_sample, kv128_p, kv512_p, kv2048_p, conv_p, kv128_s, kv512_s, kv2048_s, conv_s)
```

```python
import math
import numpy as np
import concourse.bass as bass
import concourse.mybir as mybir
from concourse.bass_utils import run_bass_kernel_spmd

F32, BF16 = mybir.dt.float32, mybir.dt.bfloat16
AF = mybir.ActivationFunctionType
ALU = mybir.AluOpType
AX = mybir.AxisListType

D = 1024
KC = 8
SEQ = 4096
NS = 16
FF = 2816
NJ = 22
EPS = 1e-6
GROUPS = ((128, 1), (512, 4), (2048, 16))
CW = 31
SELF_SYNC = True
SB_BASE = 16512
SB_END = 229344


class Op:
    __slots__ = ("eng", "fn", "dma", "deps", "signal", "count", "sem", "target", "gidx")

    def __init__(self, eng, fn, dma, gidx):
        self.eng, self.fn, self.dma, self.gidx = eng, fn, dma, gidx
        self.deps = set()
        self.signal = False
        self.count = 0
        self.sem = None
        self.target = 0


class Prog:
    ENGS = ("pe", "act", "dve", "pool", "sp")

    def __init__(self):
        self.ops = {e: [] for e in self.ENGS}
        self.last_w = {}
        self.readers = {}
        self.n = 0
        self.bar = None

    def add(self, eng, fn, reads=(), writes=(), dma=False):
        op = Op(eng, fn, dma, self.n)
        self.n += 1
        ps_r = tuple(r for r in reads if r.startswith("ps_"))
        if ps_r:
            reads = tuple(r for r in reads if not r.startswith("ps_"))
            writes = tuple(writes) + ps_r
        deps = op.deps
        for r in reads:
            w = self.last_w.get(r, self.bar)
            if w is not None:
                deps.add(w)
        for w_ in writes:
            w = self.last_w.get(w_, self.bar)
            if w is not None:
                deps.add(w)
            rd = self.readers.get(w_)
            if rd:
                for o in rd[0].values():
                    deps.add(o)
                for o in rd[1]:
                    deps.add(o)
        deps.discard(op)
        for d in deps:
            d.signal = True
        for r in reads:
            rd = self.readers.get(r)
            if rd is None:
                rd = self.readers[r] = [{}, []]
            if dma:
                rd[1].append(op)
            else:
                rd[0][eng] = op
        for w_ in writes:
            self.last_w[w_] = op
            self.readers[w_] = [{}, []]
        self.ops[eng].append(op)
        return op

    def barrier(self, fn):
        names = tuple(set(self.last_w.keys()) | set(self.readers.keys()))
        op = self.add("dve", fn, (), names)
        self.bar = op
        return op

    def prepare(self, cnt_sems, dma_pools):
        self.cnt_sems = cnt_sems
        for e in self.ENGS:
            c = 0
            uses = {}
            pool = dma_pools.get(e, [])
            ndma = 0
            for op in self.ops[e]:
                if op.dma:
                    s = pool[ndma % len(pool)]
                    ndma += 1
                    uses[s] = uses.get(s, 0) + 1
                    op.sem = s
                    op.target = 16 * uses[s]
                elif op.signal:
                    c += 1
                    op.count = c
        self.final = {}
        for e in self.ENGS:
            for op in self.ops[e]:
                if op.dma:
                    self.final[op.sem] = max(self.final.get(op.sem, 0), op.target)

    def emit_one(self, e, eng):
        cnt_sems = self.cnt_sems
        waited = {}
        nw = 0
        for op in self.ops[e]:
            waits = {}
            for d in op.deps:
                if d.dma:
                    waits[d.sem] = max(waits.get(d.sem, 0), d.target)
                else:
                    if d.eng == e and (e == "pe" or not SELF_SYNC):
                        continue
                    s = cnt_sems[d.eng]
                    waits[s] = max(waits.get(s, 0), d.count)
            if op.dma and op.target > 16:
                waits[op.sem] = max(waits.get(op.sem, 0), op.target - 16)
            for s, v in waits.items():
                if waited.get(s, 0) < v:
                    eng.wait_ge(s, v)
                    waited[s] = v
                    nw += 1
            ins = op.fn(eng)
            if op.dma:
                ins.then_inc(op.sem, 16)
            elif op.signal:
                ins.then_inc(cnt_sems[e], 1)
        if e == "sp":
            for s, v in self.final.items():
                if waited.get(s, 0) < v:
                    eng.wait_ge(s, v)
        return nw


class Arena:
    def __init__(self, nc, lo, hi):
        self.nc, self.lo, self.hi, self.off = nc, lo, hi, lo
        self.k = 0

    def alloc(self, name, shape, dtype):
        nbytes = int(np.prod(shape[1:])) * (2 if dtype == BF16 else 4)
        nbytes = (nbytes + 63) // 64 * 64
        assert self.off + nbytes <= self.hi, (name, self.off, nbytes, self.hi)
        self.k += 1
        t = self.nc.alloc_sbuf_tensor_at(f"{name}_{self.k}", list(shape), dtype, offset=self.off)
        self.off += nbytes
        return t

    def mark(self):
        return self.off

    def release(self, m):
        self.off = m


def pieces(t0, t1, step=512):
    out = []
    t = t0
    while t < t1:
        n = min(step, t1 - t)
        out.append((t, n))
        t += n
    return out


PHASES = []
MMCOUNT = [0]


class _Stop(Exception):
    pass


def build(dbg=False, stop=99):
    nc = bass.Bass("TRN2", target_bir_lowering=False)
    P = Prog()

    def ckpt(k):
        PHASES.append((str(k), MMCOUNT[0]))
        if stop == k:
            raise _Stop()

    def din(name, shape, dt=F32):
        return nc.dram_tensor(name, list(shape), dt, kind="ExternalInput").ap()

    def dout(name, shape, dt=F32):
        return nc.dram_tensor(name, list(shape), dt, kind="ExternalOutput").ap()

    xT = din("xT", [D, SEQ]).rearrange("(k p) t -> p k t", p=128)
    xsT = din("xsT", [D, NS]).rearrange("(k p) t -> p k t", p=128)
    cT = din("cT", [D, 1 + NS]).rearrange("(k p) t -> p k t", p=128)
    w_mod = din("w_mod", [2, D, 6 * D])
    vecs = din("vecs", [128, 256])
    flag = din("flag", [128, 1])
    wqkv = din("wqkv", [24, 128, KC, 384])
    ebias = din("ebias", [24, 128, 3, 128])
    w_o = din("w_o", [D, D]).rearrange("(s p) f -> p s f", p=128)
    w_gate = din("w_gate", [2, D, FF])
    w_up = din("w_up", [2, D, FF])
    w_down = din("w_down", [2, KC, 128, NJ, 128])
    w_pw1 = din("w_pw1", [D, 2 * D]).rearrange("(k p) f -> p k f", p=128)
    w_pw2 = din("w_pw2", [D, D]).rearrange("(k p) f -> p k f", p=128)
    wdwT = din("wdwT", [128, KC, CW])
    ck = [din(f"ck{g}", [NS, 2, GROUPS[g][0], D]) for g in range(3)]
    sbias = din("sbias", [128, 24])
    sbias0 = din("sbias0", [128, 24])
    stT = din("stT", [128, KC, NS, CW - 1])
    identd = din("identd", [128, 128])

    yT = dout("yT", [D, 2048]).rearrange("(k p) t -> p k t", p=128)
    ysT = dout("ysT", [D, NS]).rearrange("(k p) t -> p k t", p=128)
    koutT = [dout(f"koutT{g}", [8, 128, GROUPS[g][0]]) for g in range(3)]
    vout = [dout(f"vout{g}", [GROUPS[g][0], D]) for g in range(3)]
    convT = dout("convT", [128, KC, CW - 1])
    ksT = dout("ksT", [128, 24, NS])
    vsT = dout("vsT", [128, 24, NS])
    convsT = dout("convsT", [128, KC, NS, CW - 1])
    dbgT = dout("dbgT", [D, 2192]).rearrange("(k p) t -> p k t", p=128) if dbg else None

    def dump(name, ap, shape, dt, res):
        if not dbg:
            return
        d = nc.dram_tensor("dbg_" + name, list(shape), dt, kind="ExternalOutput").ap()
        dma("sp", d, ap, res, ())

    pers = Arena(nc, SB_BASE, SB_BASE + 14 * 1024)
    lo = Arena(nc, SB_BASE + 14 * 1024, SB_BASE + 50 * 1024)
    hi = Arena(nc, SB_BASE + 50 * 1024, SB_END)

    ident = pers.alloc("ident", [128, 128], BF16)
    ones_b = pers.alloc("ones_b", [128, 128], BF16)
    ones_f = pers.alloc("ones_f", [128, 128], F32)
    vec = pers.alloc("vec", [128, 256], F32)
    flg = pers.alloc("flg", [128, 1], F32)
    epsc = pers.alloc("epsc", [128, 1], F32)
    bscr = pers.alloc("bscr", [128, 1], F32)
    pbias = pers.alloc("pbias", [128, 512], F32)
    csT = pers.alloc("csT", [128, KC, 1 + NS], F32)
    modT = pers.alloc("modT", [128, 2, 48, 1 + NS], F32)
    Acol = pers.alloc("Acol", [128, 4, KC], F32)
    As = pers.alloc("As", [128, 4, KC, NS], F32)
    V_BMOD, V_GMIX, V_GFFN, V_GFIN = 0, 96, 112, 128
    V_BPW1, V_BDW, V_LNG, V_LNB, V_BPW2 = 136, 152, 160, 168, 176

    psb = [nc.alloc_psum_tensor(f"ps_b{i}", [128, 512], F32) for i in range(8)]
    ps_s = [psb[4], psb[5]]
    ps_nd = [psb[6], psb[7]]
    mm_pool = [[0, 1, 2, 3]]
    mmctr = [0]

    def mmbank():
        pl = mm_pool[0]
        i = pl[mmctr[0] % len(pl)]
        mmctr[0] += 1
        return psb[i], f"ps_b{i}"

    def mm_group(out_ap, pairs, reads, writes):
        MMCOUNT[0] += len(pairs)

        def fn(pe, pairs=pairs, out_ap=out_ap):
            ins = None
            n = len(pairs)
            for i, (l, r) in enumerate(pairs):
                ins = pe.matmul(out_ap, l, r, start=(i == 0), stop=(i == n - 1))
            return ins
        return P.add("pe", fn, reads, writes)

    def act(out_ap, in_ap, func, reads, writes, bias=None, scale=None):
        kw = {}
        if bias is not None:
            kw["bias"] = bias
        if scale is not None:
            kw["scale"] = scale
        return P.add("act", lambda e: e.activation(out_ap, in_ap, func, **kw), reads, writes)

    def dve_tt(out_ap, a, b, op, reads, writes, eng="dve"):
        return P.add(eng, lambda e: e.tensor_tensor(out_ap, a, b, op), reads, writes)

    def dve_ts(out_ap, a, s1, s2, op0, op1, reads, writes, eng="dve"):
        if op1 is None:
            return P.add(eng, lambda e: e.tensor_scalar(out_ap, a, s1, None, op0), reads, writes)
        return P.add(eng, lambda e: e.tensor_scalar(out_ap, a, s1, s2, op0, op1), reads, writes)

    def dve_stt(out_ap, a, s, b, op0, op1, reads, writes, eng="dve"):
        return P.add(eng, lambda e: e.scalar_tensor_tensor(out_ap, a, s, b, op0, op1), reads, writes)

    def dma(q, out_ap, in_ap, reads, writes):
        return P.add(q, lambda e: e.dma_start(out=out_ap, in_=in_ap), reads, writes, dma=True)

    try:
        P.add("pool", lambda e: e.memset(ones_b[:], 1.0), (), ("ones_b",))
        P.add("pool", lambda e: e.memset(ones_f[:], 1.0), (), ("ones_f",))
        P.add("pool", lambda e: e.memset(epsc[:], D * EPS), (), ("epsc",))
        dma("sp", vec[:], vecs, (), ("vec",))
        dma("pool", ident[:], identd, (), ("ident",))
        dma("sp", flg[:], flag, (), ("flg",))
        dma("sp", csT[:], cT, (), ("csT",))
        act(csT[:], csT[:], AF.Silu, ("csT",), ("csT",))
        ckpt(1)

        m_hiA = hi.mark()
        csb = hi.alloc("csb", [128, KC, 1 + NS], BF16)
        hT = hi.alloc("hT", [128, KC, SEQ + NS], BF16)
        m_hiA2 = hi.mark()
        wst = [lo.alloc(f"wst{i}", [128, KC, 512], BF16) for i in range(2)]
        P.add("dve", lambda e: e.tensor_copy(csb[:], csT[:]), ("csT",), ("csb",))
        modit = [0]

        def mres(l, j):
            return f"modT{l}g{j // 8}"

        def mod_tile(l, ft, width=512, bufs=None, bname="wst"):
            bufs = bufs or wst
            wv = w_mod[l].rearrange("(k p) f -> p k f", p=128)
            s = modit[0] % 2
            modit[0] += 1
            nfc = width // 128
            dma("pool", bufs[s][:, :, :width], wv[:, :, ft * width:(ft + 1) * width], (), (f"{bname}{s}",))
            pb, pbn = mmbank()
            for fc in range(nfc):
                pairs = [(bufs[s][:, k, fc * 128:(fc + 1) * 128], csb[:, k, :]) for k in range(KC)]
                mm_group(pb[:, fc * 32:fc * 32 + 1 + NS], pairs, (f"{bname}{s}", "csb"), (pbn,))
            for fc in range(nfc):
                j = ft * nfc + fc
                act(modT[:, l, j, :], pb[:, fc * 32:fc * 32 + 1 + NS], AF.Identity, (pbn, "vec"), (mres(l, j),),
                    bias=vec[:, V_BMOD + l * 48 + j:V_BMOD + l * 48 + j + 1])

        def scale_cols(l, wh):
            gcol = (V_GMIX if wh == 0 else V_GFFN) + l * 8
            sc0 = 8 if wh == 0 else 32
            idx = l * 2 + wh
            dve_ts(Acol[:, idx, :], modT[:, l, sc0:sc0 + 8, 0], 1.0, 32.0, ALU.add, ALU.mult,
                   (mres(l, sc0),), (f"Acol{idx}",))
            dve_tt(Acol[:, idx, :], Acol[:, idx, :], vec[:, gcol:gcol + 8], ALU.mult, (f"Acol{idx}", "vec"),
                   (f"Acol{idx}",))
            dve_ts(As[:, idx, :, :], modT[:, l, sc0:sc0 + 8, 1:], 1.0, 32.0, ALU.add, ALU.mult,
                   (mres(l, sc0),), (f"As{idx}",))
            dve_tt(As[:, idx, :, :], As[:, idx, :, :],
                   vec[:, gcol:gcol + 8].unsqueeze(2).broadcast_to([128, 8, NS]), ALU.mult,
                   (f"As{idx}", "vec"), (f"As{idx}",))

        rest_tiles = [(0, ft) for ft in range(8, 24)] + [(1, ft) for ft in range(24)]
        for ft in range(4):
            mod_tile(0, ft)
        scale_cols(0, 0)

        dump("csT", csT[:], [128, KC, 1 + NS], F32, ("csT",))
        ckpt(2)
        NB = {}

        def alloc_norm(width):
            NB["sq"] = [hi.alloc(f"sq{i}", [128, KC, width], BF16) for i in range(2)]
            NB["tmp"] = [hi.alloc(f"tmp{i}", [128, KC, width], F32) for i in range(2)]
            NB["rstd"] = [hi.alloc(f"rstd{i}", [128, width], F32) for i in range(2)]
            NB["i"] = 0

        NBP = [None]

        def norm_flush():
            f_ = NBP[0]
            NBP[0] = None
            if f_ is not None:
                f_()

        def norm_stage(x_ap, xres, n, mult_eng, finish):
            b_ = NB["i"] % 2
            NB["i"] += 1
            sq, tmp, rstd = NB["sq"][b_], NB["tmp"][b_], NB["rstd"][b_]
            dve_tt(sq[:, :, :n], x_ap, x_ap, ALU.mult, xres, (f"sq{b_}",))
            pb, pbn = mmbank()
            mm_group(pb[:, :n], [(ones_b[:], sq[:, k, :n]) for k in range(KC)], (f"sq{b_}", "ones_b"), (pbn,))

            def stage_b():
                act(rstd[:, :n], pb[:, :n], AF.Ln, (pbn, "epsc"), (f"rstd{b_}",), bias=epsc[:, 0:1])
                act(rstd[:, :n], rstd[:, :n], AF.Exp, (f"rstd{b_}",), (f"rstd{b_}",), scale=-0.5)
                dve_tt(tmp[:, :, :n], x_ap, rstd[:, :n].unsqueeze(1).broadcast_to([128, KC, n]), ALU.mult,
                       tuple(xres) + (f"rstd{b_}",), (f"tmp{b_}",), eng=mult_eng)
                finish(tmp, f"tmp{b_}")

            prev = NBP[0]
            NBP[0] = stage_b
            if prev is not None:
                prev()

        def norm_prompt(x_ap, xres, out_ap, outres, n, idx, l, sh0, mult_eng="pool"):
            def fin(tmp, tn):
                for k in range(KC):
                    act(out_ap[:, k, :], tmp[:, k, :n], AF.Identity, (tn, f"Acol{idx}", mres(l, sh0)), outres,
                        bias=modT[:, l, sh0 + k, 0:1], scale=Acol[:, idx, k:k + 1])
            norm_stage(x_ap, xres, n, mult_eng, fin)

        def norm_sample(x_ap, xres, out_ap, outres, idx, l, sh0):
            n = NS

            def fin(tmp, tn):
                dve_tt(tmp[:, :, :n], tmp[:, :, :n], As[:, idx, :, :], ALU.mult, (tn, f"As{idx}"), (tn,))
                dve_tt(out_ap, tmp[:, :, :n], modT[:, l, sh0:sh0 + 8, 1:], ALU.add, (tn, mres(l, sh0)), outres)
            norm_stage(x_ap, xres, n, "pool", fin)
            norm_flush()

        xstage = [hi.alloc(f"xstage{i}", [128, KC, 512], F32) for i in range(2)]
        alloc_norm(512)
        for tt in range(8):
            s = tt % 2
            t0 = tt * 512
            dma("sp", xstage[s][:], xT[:, :, t0:t0 + 512], (), (f"xstage{s}",))
            norm_prompt(xstage[s][:], (f"xstage{s}",), hT[:, :, t0:t0 + 512], (f"hT{tt}",), 512, 0, 0, 0, "dve")
        dma("sp", xstage[0][:, :, :NS], xsT, (), ("xstage0",))
        norm_sample(xstage[0][:, :, :NS], ("xstage0",), hT[:, :, SEQ:SEQ + NS], ("hTs",), 0, 0, 0)
        hT_all_res = tuple(f"hT{i}" for i in range(8))
        dump("hT", hT[:, :, 2048:2304], [128, KC, 256], BF16, hT_all_res)
        dump("hTs", hT[:, :, SEQ:SEQ + NS], [128, KC, NS], BF16, ("hTs",))
        ckpt(3)

        hi.release(m_hiA2)
        lo.release(lo.lo)
        P.barrier(lambda e: e.memset(bscr[:], 0.0))
        OT = lo.alloc("OT", [128, 8, 2192], BF16)
        qs_st = hi.alloc("qs_st", [128, 24, NS], BF16)
        ks_bf = hi.alloc("ks_bf", [128, 24, NS], BF16)
        ks_st = hi.alloc("ks_st", [128, 24, NS], F32)
        vs_st = hi.alloc("vs_st", [128, 24, NS], F32)
        m_hiS = hi.mark()
        QT = hi.alloc("QT", [128, SEQ], BF16)
        KT = hi.alloc("KT", [128, SEQ], BF16)
        Vb = hi.alloc("Vb", [128, 32, 128], BF16)
        wq = [hi.alloc(f"wq{i}", [128, KC, 384], BF16) for i in range(2)]
        Eb = [hi.alloc(f"Eb{i}", [128, 3, 128], F32) for i in range(2)]
        Pf = [hi.alloc(f"Pf{i}", [128, 512], F32) for i in range(2)]
        Pb = [hi.alloc(f"Pb{i}", [128, 512], BF16) for i in range(2)]
        Nacc = hi.alloc("Nacc", [128, 2176], F32)
        Dsl = hi.alloc("Dsl", [1, 2176], F32)
        wst2 = [hi.alloc(f"wsu{i}", [128, KC, 256], BF16) for i in range(2)]
        kst = [hi.alloc(f"kst{i}", [128, 512], F32) for i in range(2)]
        vst = [hi.alloc(f"vst{i}", [128, 4, 128], F32) for i in range(2)]
        QSCALE = 1.0 / math.sqrt(128.0)
        kstc = [0]
        vstc = [0]
        sbc = [0]

        units = [(slot, g) for slot in range(8) for g in range(3)]

        def load_unit(ui):
            slot, g = units[ui]
            b = ui % 2
            dma("pool", wq[b][:], wqkv[g * 8 + slot], (), (f"wq{b}",))
            dma("sp", Eb[b][:], ebias[g * 8 + slot], (), (f"Eb{b}",))
            act(Eb[b][:], Eb[b][:], AF.Exp, (f"Eb{b}",), (f"Eb{b}",))

        load_unit(0)
        pending_norm = [None]
        for ui, (slot, g) in enumerate(units):
            if ui + 1 < len(units):
                load_unit(ui + 1)
            for _ in range(2):
                if rest_tiles:
                    mod_tile(*rest_tiles.pop(0), width=256, bufs=wst2, bname="wsu")
            b = ui % 2
            W, r = GROUPS[g]
            A = 32 // r
            a_k0 = {0: 14, 1: 2, 2: 0}[g]
            a_own0 = A // 2
            wres = (f"wq{b}",)
            QTv = QT[:].rearrange("p (res m) -> p res m", res=r)
            KTv = KT[:].rearrange("p (res m) -> p res m", res=r)

            def hres(t0, n):
                return tuple(sorted({f"hT{t0 // 512}", f"hT{(t0 + n - 1) // 512}"}))

            for (t0, n) in pieces(1920, 2048) + pieces(2048, SEQ):
                pb, pbn = mmbank()
                mm_group(pb[:, :n], [(wq[b][:, k, 0:128], hT[:, k, t0:t0 + n]) for k in range(KC)],
                         wres + hres(t0, n), (pbn,))
                act(QTv[:, :, t0 // r:(t0 + n) // r], pb[:, :n].rearrange("p (m res) -> p res m", res=r),
                    AF.Copy, (pbn,), ("QT",), scale=QSCALE)
            if ui == 0:
                ckpt(41)
            for (t0, n) in pieces(a_k0 * 128 * r, SEQ):
                pb, pbn = mmbank()
                mm_group(pb[:, :n], [(wq[b][:, k, 128:256], hT[:, k, t0:t0 + n]) for k in range(KC)],
                         wres + hres(t0, n), (pbn,))
                P.add("dve", lambda e, o=KTv[:, :, t0 // r:(t0 + n) // r],
                      i=pb[:, :n].rearrange("p (m res) -> p res m", res=r): e.tensor_copy(o, i), (pbn,), ("KT",))
                o0 = max(t0, SEQ - W)
                if o0 < t0 + n:
                    kb_ = kstc[0] % 2
                    kstc[0] += 1
                    nn = t0 + n - o0
                    act(kst[kb_][:, :nn], pb[:, o0 - t0:n], AF.Copy, (pbn,), (f"kst{kb_}",))
                    dma("sp", koutT[g][slot, :, o0 - (SEQ - W):o0 - (SEQ - W) + nn], kst[kb_][:, :nn],
                        (f"kst{kb_}",), ())
            if ui == 0:
                ckpt(42)
            pb, pbn = mmbank()
            for t in range(3):
                mm_group(pb[:, t * 32:t * 32 + NS],
                         [(wq[b][:, k, t * 128:(t + 1) * 128], hT[:, k, SEQ:SEQ + NS]) for k in range(KC)],
                         wres + ("hTs",), (pbn,))
            ui_ = g * 8 + slot
            act(qs_st[:, ui_, :], pb[:, 0:NS], AF.Copy, (pbn,), ("qs_st",), scale=QSCALE)
            act(ks_st[:, ui_, :], pb[:, 32:32 + NS], AF.Copy, (pbn,), ("ks_st",))
            act(vs_st[:, ui_, :], pb[:, 64:64 + NS], AF.Copy, (pbn,), ("vs_st",))
            if ui == 0:
                ckpt(43)
            kbl = [(res, a) for res in range(r) for a in range(a_k0, A)]

            def vidx(res, a):
                return res * (A - a_k0) + (a - a_k0)

            for q0 in range(0, len(kbl), 4):
                grp = kbl[q0:q0 + 4]
                pb, pbn = mmbank()
                rr = set()
                for qi, (res, a) in enumerate(grp):
                    tlo = (128 * a) * r + res
                    cols = slice(tlo, tlo + 127 * r + 1, r)
                    rr |= set(hres(128 * a * r, 128 * r))
                    mm_group(pb[:, qi * 128:(qi + 1) * 128],
                             [(hT[:, k, cols], wq[b][:, k, 256:384]) for k in range(KC)],
                             wres + tuple(rr), (pbn,))
                ng = len(grp)
                P.add("dve", lambda e, o=Vb[:, q0:q0 + ng, :],
                      i=pb[:, :ng * 128].rearrange("p (q d) -> p q d", q=ng): e.tensor_copy(o, i), (pbn,), ("Vb",))
                outs = [(qi, res, a) for qi, (res, a) in enumerate(grp) if a == A - 1]
                if outs:
                    vb_ = vstc[0] % 2
                    vstc[0] += 1
                    act(vst[vb_][:, :ng, :], pb[:, :ng * 128].rearrange("p (q d) -> p q d", q=ng), AF.Copy,
                        (pbn,), (f"vst{vb_}",))
                    for (qi, res, a) in outs:
                        dst = vout[g][res:res + 127 * r + 1:r, slot * 128:(slot + 1) * 128]
                        dma("sp", dst, vst[vb_][:, qi, :], (f"vst{vb_}",), ())
            if ui == 0:
                ckpt(4)
            if pending_norm[0] is not None:
                pending_norm[0]()
                pending_norm[0] = None
            banks = []
            for res in range(r):
                a_h = a_own0 - 1
                i0 = 128 - 128 // r
                halo = (res, a_h, i0, 0 if a_h >= 1 else None)
                first = (res, a_own0, 0, 2)
                banks.append([halo, first])
                rest = [(res, a, 0, 0) for a in range(a_own0 + 1, A)]
                for j in range(0, len(rest), 2):
                    banks.append(rest[j:j + 2])
            def stage1(bk):
                sb_ = sbc[0] % 2
                sbc[0] += 1
                S, Sn = ps_s[sb_], f"ps_b{4 + sb_}"
                Nn_ = Dn_ = f"ps_b{6 + sb_}"
                NP = ps_nd[sb_]
                DP = ps_nd[sb_][:, 256:512]
                col = 0
                qcol = 0
                lay = []
                for (res, a, i0, et) in bk:
                    n = 128 - i0
                    subs = []
                    if et is not None:
                        subs.append((et, a - 1, col))
                        col += n
                    subs.append((1, a, col))
                    col += n
                    lay.append(((res, a, i0, et), subs, n, qcol))
                    qcol += n
                full_pair = (len(lay) == 2 and all(l_[0][2] == 0 and l_[0][3] == 0 for l_ in lay))
                if full_pair:
                    res, a = lay[0][0][0], lay[0][0][1]
                    mm_group(S[:, 0:128], [(KTv[:, res, (a - 1) * 128:a * 128], QTv[:, res, a * 128:(a + 1) * 128])],
                             ("KT", "QT"), (Sn,))
                    mm_group(S[:, 128:384], [(KTv[:, res, a * 128:(a + 1) * 128], QTv[:, res, a * 128:(a + 2) * 128])],
                             ("KT", "QT"), (Sn,))
                    mm_group(S[:, 384:512], [(KTv[:, res, (a + 1) * 128:(a + 2) * 128],
                                              QTv[:, res, (a + 1) * 128:(a + 2) * 128])], ("KT", "QT"), (Sn,))
                else:
                    for (res, a, i0, et), subs, n, qc in lay:
                        for (kind, ka, sc) in subs:
                            mm_group(S[:, sc:sc + n],
                                     [(KTv[:, res, ka * 128:(ka + 1) * 128], QTv[:, res, a * 128 + i0:a * 128 + 128])],
                                     ("KT", "QT"), (Sn,))
                act(Pf[sb_][:, :col], S[:, :col], AF.Exp, (Sn,), (f"Pf{sb_}",))
                if full_pair:
                    dve_tt(Pb[sb_][:, :512].rearrange("p (s e q) -> p s e q", s=2, e=2),
                           Pf[sb_][:, :512].rearrange("p (s e q) -> p s e q", s=2, e=2),
                           Eb[b][:, 0:2, :].unsqueeze(1).broadcast_to([128, 2, 2, 128]), ALU.mult,
                           (f"Pf{sb_}", f"Eb{b}"), (f"Pb{sb_}",))
                else:
                    for (res, a, i0, et), subs, n, qc in lay:
                        for (kind, ka, sc) in subs:
                            dve_tt(Pb[sb_][:, sc:sc + n], Pf[sb_][:, sc:sc + n], Eb[b][:, kind, i0:128], ALU.mult,
                                   (f"Pf{sb_}", f"Eb{b}"), (f"Pb{sb_}",))
                return (sb_, lay, full_pair, NP, DP, Nn_, Dn_)

            def stage2(info):
                sb_, lay, full_pair, NP, DP, Nn_, Dn_ = info
                for (res, a, i0, et), subs, n, qc in lay:
                    pv = [(Vb[:, vidx(res, ka), :], Pb[sb_][:, sc:sc + n]) for (kind, ka, sc) in subs]
                    mm_group(NP[:, qc:qc + n], pv, ("Vb", f"Pb{sb_}"), (Nn_,))
                    dd = [(ones_b[:, 0:1], Pb[sb_][:, sc:sc + n]) for (kind, ka, sc) in subs]
                    mm_group(DP[0:1, qc:qc + n], dd, ("ones_b", f"Pb{sb_}"), (Dn_,))
                if full_pair:
                    (res, a, i0, et), subs, n, qc = lay[0]
                    ev = [(res, a, 0, 256, 0)]
                else:
                    ev = [(res, a, i0, n, qc) for (res, a, i0, et), subs, n, qc in lay]
                for (res, a, i0, n, qc) in ev:
                    base = (128 * a + i0) * r + res - 1920
                    cs = slice(base, base + (n - 1) * r + 1, r)
                    if g == 0:
                        act(Nacc[:, cs], NP[:, qc:qc + n], AF.Copy, (Nn_,), ("Nacc",))
                        act(Dsl[0:1, cs], DP[0:1, qc:qc + n], AF.Copy, (Dn_,), ("Dsl",))
                    else:
                        dve_tt(Nacc[:, cs], NP[:, qc:qc + n], Nacc[:, cs], ALU.add, (Nn_, "Nacc"), ("Nacc",))
                        dve_tt(Dsl[0:1, cs], DP[0:1, qc:qc + n], Dsl[0:1, cs], ALU.add, (Dn_, "Dsl"), ("Dsl",))
            pend = None
            for bk in banks:
                info = stage1(bk)
                if pend is not None:
                    stage2(pend)
                pend = info
            stage2(pend)
            if ui == 0:
                ckpt(5)
            if g == 2:
                def do_norm(slot=slot):
                    P.add("dve", lambda e: e.reciprocal(Dsl[:], Dsl[:]), ("Dsl",), ("Dsl",))
                    for (c0, n) in pieces(0, 2176):
                        pb, pbn = mmbank()
                        mm_group(pb[:, :n], [(ones_f[0:1, :], Dsl[0:1, c0:c0 + n])], ("ones_f", "Dsl"), (pbn,))
                        dve_tt(OT[:, slot, c0:c0 + n], Nacc[:, c0:c0 + n], pb[:, :n], ALU.mult, ("Nacc", pbn),
                               (f"OT{slot}",))
                pending_norm[0] = do_norm
        if pending_norm[0] is not None:
            pending_norm[0]()
            pending_norm[0] = None
        while rest_tiles:
            mod_tile(*rest_tiles.pop(0), width=256, bufs=wst2, bname="wsu")
        scale_cols(0, 1)
        scale_cols(1, 0)
        scale_cols(1, 1)

        if dbg:
            for slot in range(8):
                pass

        dma("sp", ksT, ks_st[:], ("ks_st",), ())
        dma("sp", vsT, vs_st[:], ("vs_st",), ())
        ckpt(55)
        hi.release(m_hiS)
        P.barrier(lambda e: e.memset(bscr[:], 0.0))
        Kc = [hi.alloc(f"Kc{i}", [128, D], BF16) for i in range(4)]
        Vc = [hi.alloc(f"Vc{i}", [128, D], BF16) for i in range(4)]
        KcT = hi.alloc("KcT", [128, 8, 128], BF16)
        sPf = hi.alloc("sPf", [128, 8], F32)
        sPb = hi.alloc("sPb", [128, 8], BF16)
        sb_sb = hi.alloc("sb_sb", [128, 24], F32)
        sb0_sb = hi.alloc("sb0_sb", [128, 24], F32)
        Nsa = hi.alloc("Nsa", [128, 3, 8, NS], F32)
        Dsa = hi.alloc("Dsa", [1, 3, 8, NS], F32)
        prod = hi.alloc("prod", [128, 24, NS], F32)
        p0 = hi.alloc("p0", [1, 24, NS], F32)
        Dtot = hi.alloc("Dtot", [1, 8, NS], F32)
        Ntot = hi.alloc("Ntot", [128, 8, NS], F32)
        dma("sp", sb_sb[:], sbias, (), ("sb_sb",))
        dma("sp", sb0_sb[:], sbias0, (), ("sb0_sb",))
        def sLoad(g, b_, ik):
            r = GROUPS[g][1]
            dma("pool", Kc[ik][:], ck[g][b_, 0, 0:128 * r:r, :], (), (f"Kc{ik}",))
            dma("pool", Vc[ik][:], ck[g][b_, 1, 0:128 * r:r, :], (), (f"Vc{ik}",))

        def sA(g, b_, i, ik):
            for hq in range(2):
                pb, pbn = mmbank()
                for hh in range(4):
                    h = hq * 4 + hh
                    mm_group(pb[:, hh * 128:(hh + 1) * 128], [(Kc[ik][:, h * 128:(h + 1) * 128], ident[:])],
                             (f"Kc{ik}", "ident"), (pbn,))
                P.add("dve", lambda e, o=KcT2[i][:, hq * 4:(hq + 1) * 4, :],
                      i_=pb[:, :512].rearrange("p (h j) -> p h j", h=4): e.tensor_copy(o, i_), (pbn,), (f"KcT{i}",))
            for h in range(8):
                mm_group(ps_s[i][:, h:h + 1], [(KcT2[i][:, h, :], qs_st[:, g * 8 + h, b_:b_ + 1])],
                         (f"KcT{i}", "qs_st"), (f"ps_b{4 + i}",))
            dve_tt(sPf2[i][:], ps_s[i][:, 0:8], sb_sb[:, g * 8:(g + 1) * 8], ALU.add, (f"ps_b{4 + i}", "sb_sb"),
                   (f"sPf{i}",))
            act(sPb2[i][:], sPf2[i][:], AF.Exp, (f"sPf{i}",), (f"sPb{i}",))

        def sB(g, b_, i, ik):
            for h in range(8):
                mm_group(ps_nd[i][:, h:h + 1], [(Vc[ik][:, h * 128:(h + 1) * 128], sPb2[i][:, h:h + 1])],
                         (f"Vc{ik}", f"sPb{i}"), (f"ps_b{6 + i}",))
            mm_group(ps_nd[i][0:1, 256:264], [(ones_b[:, 0:1], sPb2[i][:, 0:8])], ("ones_b", f"sPb{i}"),
                     (f"ps_b{6 + i}",))
            act(Nsa[:, g, :, b_], ps_nd[i][:, 0:8], AF.Copy, (f"ps_b{6 + i}",), ("Nsa",))
            act(Dsa[0:1, g, :, b_], ps_nd[i][0:1, 256:264], AF.Copy, (f"ps_b{6 + i}",), ("Dsa",))

        KcT2 = [KcT, hi.alloc("KcTb", [128, 8, 128], BF16)]
        sPf2 = [sPf, hi.alloc("sPfb", [128, 8], F32)]
        sPb2 = [sPb, hi.alloc("sPbb", [128, 8], BF16)]
        mm_pool[0] = [0, 1, 2, 3]
        ulist = [(g, b_) for g in range(3) for b_ in range(NS)]
        pend = None
        sLoad(*ulist[0], 0)
        sLoad(*ulist[1], 1)
        for ui2, (g, b_) in enumerate(ulist):
            if ui2 + 2 < len(ulist):
                sLoad(*ulist[ui2 + 2], (ui2 + 2) % 4)
            sA(g, b_, ui2 % 2, ui2 % 4)
            if pend is not None:
                sB(*pend)
            pend = (g, b_, ui2 % 2, ui2 % 4)
        sB(*pend)
        dve_tt(prod[:], qs_st[:], ks_st[:], ALU.mult, ("qs_st", "ks_st"), ("prod",))
        pb, pbn = mmbank()
        mm_group(pb[0:1, :24 * NS], [(ones_f[:, 0:1], prod[:].rearrange("p a b -> p (a b)"))], ("ones_f", "prod"), (pbn,))
        dve_tt(p0[:], pb[0:1, :24 * NS].rearrange("p (a b) -> p a b", b=NS),
               sb0_sb[0:1, :].unsqueeze(2).broadcast_to([1, 24, NS]), ALU.add, (pbn, "sb0_sb"), ("p0",))
        act(p0[:], p0[:], AF.Exp, ("p0",), ("p0",))
        pb, pbn = mmbank()
        mm_group(pb[:, :24 * NS], [(ones_f[0:1, :], p0[:].rearrange("p a b -> p (a b)"))], ("ones_f", "p0"), (pbn,))
        dve_tt(prod[:], pb[:, :24 * NS].rearrange("p (a b) -> p a b", b=NS), vs_st[:], ALU.mult, (pbn, "vs_st"), ("prod",))
        dve_tt(Ntot[:], Nsa[:, 0], prod[:, 0:8, :], ALU.add, ("Nsa", "prod"), ("Ntot",))
        dve_tt(Dtot[:], Dsa[0:1, 0], p0[0:1, 0:8, :], ALU.add, ("Dsa", "p0"), ("Dtot",))
        for g in (1, 2):
            dve_tt(Ntot[:], Ntot[:], Nsa[:, g], ALU.add, ("Nsa", "Ntot"), ("Ntot",))
            dve_tt(Ntot[:], Ntot[:], prod[:, g * 8:(g + 1) * 8, :], ALU.add, ("prod", "Ntot"), ("Ntot",))
            dve_tt(Dtot[:], Dtot[:], Dsa[0:1, g], ALU.add, ("Dsa", "Dtot"), ("Dtot",))
            dve_tt(Dtot[:], Dtot[:], p0[0:1, g * 8:(g + 1) * 8, :], ALU.add, ("p0", "Dtot"), ("Dtot",))
        P.add("dve", lambda e: e.reciprocal(Dtot[:], Dtot[:]), ("Dtot",), ("Dtot",))
        pb, pbn = mmbank()
        mm_group(pb[:, :8 * NS], [(ones_f[0:1, :], Dtot[:].rearrange("p a b -> p (a b)"))], ("ones_f", "Dtot"), (pbn,))
        dve_tt(OT[:, :, 2176:2176 + NS], Ntot[:], pb[:, :8 * NS].rearrange("p (a b) -> p a b", b=NS), ALU.mult,
               ("Ntot", pbn), tuple(f"OT{s_}" for s_ in range(8)))
        dump("OT", OT[:], [128, 8, 2192], BF16, tuple(f"OT{s_}" for s_ in range(8)))
        ckpt(6)
        mm_pool[0] = [0, 1, 2, 3, 4, 5, 6, 7]
        hi.release(m_hiA)
        P.barrier(lambda e: e.memset(bscr[:], 0.0))
        NCOL = 2176 + NS
        xr = hi.alloc("xr", [128, KC, NCOL], F32)
        m_hiB = hi.mark()
        wo_sb = hi.alloc("wo_sb", [128, 8, D], BF16)
        tsm = pers.alloc("tsm", [128, NS], F32)
        dma("pool", wo_sb[:], w_o, (), ("wo_sb",))
        dma("sp", xr[:, :, 0:2176], xT[:, :, 1920:SEQ], (), tuple(f"xr{f}" for f in range(KC)))
        dma("sp", xr[:, :, 2176:NCOL], xsT, (), tuple(f"xr{f}" for f in range(KC)))
        allp = pieces(0, 2176) + [(2176, NS)]

        def resid_evac(pb, pbn, f, c0, n, l, gidx, extra_bias=None):
            gj = gidx + f
            if c0 >= 2176:
                if extra_bias is not None:
                    dve_ts(tsm[:, :n], pb[:, :n], extra_bias, None, ALU.add, None, (pbn, "vec"), ("tsm",))
                    dve_tt(tsm[:, :n], tsm[:, :n], modT[:, l, gj, 1:], ALU.mult, ("tsm", mres(l, gidx)), ("tsm",))
                else:
                    dve_tt(tsm[:, :n], pb[:, :n], modT[:, l, gj, 1:], ALU.mult, (pbn, mres(l, gidx)), ("tsm",))
                dve_tt(xr[:, f, c0:c0 + n], xr[:, f, c0:c0 + n], tsm[:, :n], ALU.add, ("tsm", f"xr{f}"), (f"xr{f}",))
            else:
                if extra_bias is not None:
                    dve_ts(pbias[:, :n], pb[:, :n], extra_bias, modT[:, l, gj, 0:1], ALU.add, ALU.mult,
                           (pbn, "vec", mres(l, gidx)), ("pbias",))
                    dve_tt(xr[:, f, c0:c0 + n], xr[:, f, c0:c0 + n], pbias[:, :n], ALU.add,
                           ("pbias", f"xr{f}"), (f"xr{f}",))
                else:
                    dve_stt(xr[:, f, c0:c0 + n], pb[:, :n], modT[:, l, gj, 0:1], xr[:, f, c0:c0 + n],
                            ALU.mult, ALU.add, (pbn, mres(l, gidx), f"xr{f}"), (f"xr{f}",))

        for f in range(KC):
            for (c0, n) in allp:
                pb, pbn = mmbank()
                mm_group(pb[:, :n], [(wo_sb[:, s_, f * 128:(f + 1) * 128], OT[:, s_, c0:c0 + n]) for s_ in range(8)],
                         ("wo_sb",) + tuple(f"OT{s_}" for s_ in range(8)), (pbn,))
                resid_evac(pb, pbn, f, c0, n, 0, 16)
        dump("xa0", xr[:], [128, KC, NCOL], F32, tuple(f"xr{f}" for f in range(KC)))
        ckpt(7)

        def ffn(l, cstart):
            hi.release(m_hiB)
            lo.release(lo.lo)
            P.barrier(lambda e: e.memset(bscr[:], 0.0))
            hF = lo.alloc("hF", [128, KC, NCOL], BF16)
            alloc_norm(512)
            for (c0, n) in pieces(cstart, 2176, 512):
                norm_prompt(xr[:, :, c0:c0 + n], tuple(f"xr{f}" for f in range(KC)), hF[:, :, c0:c0 + n],
                            ("hF",), n, l * 2 + 1, l, 24)
            norm_sample(xr[:, :, 2176:NCOL], tuple(f"xr{f}" for f in range(KC)), hF[:, :, 2176:NCOL], ("hF",),
                        l * 2 + 1, l, 24)
            hi.release(m_hiB)
            P.barrier(lambda e: e.memset(bscr[:], 0.0))
            HW_ = 1168
            aT = hi.alloc("aT", [128, NJ, HW_], BF16)
            wg = [hi.alloc(f"wg{i}", [128, KC, 256], BF16) for i in range(2)]
            wu = [hi.alloc(f"wu{i}", [128, KC, 256], BF16) for i in range(2)]
            wd = [hi.alloc(f"wd{i}", [128, NJ, 128], BF16) for i in range(2)]
            gsb = [hi.alloc(f"gsb{i}", [128, 512], F32) for i in range(2)]
            halves = [pieces(cstart, 1152) + [(2176, NS)], pieces(1152, 2176)]
            wgv = w_gate[l].rearrange("(k p) f -> p k f", p=128)
            wuv = w_up[l].rearrange("(k p) f -> p k f", p=128)
            cnt_ = [0, 0, 0]
            for half in halves:
                offs = []
                o_ = 0
                for (c0, n) in half:
                    offs.append(o_)
                    o_ += n
                assert o_ <= HW_
                for jq in range(NJ // 2):
                    wb = cnt_[0] % 2
                    cnt_[0] += 1
                    dma("pool", wg[wb][:], wgv[:, :, jq * 256:(jq + 1) * 256], (), (f"wg{wb}",))
                    dma("pool", wu[wb][:], wuv[:, :, jq * 256:(jq + 1) * 256], (), (f"wu{wb}",))
                    for jj in range(2):
                        j = jq * 2 + jj
                        for (c0, n), of in zip(half, offs):
                            pg, pgn = mmbank()
                            mm_group(pg[:, :n], [(wg[wb][:, k, jj * 128:(jj + 1) * 128], hF[:, k, c0:c0 + n])
                                                 for k in range(KC)], (f"wg{wb}", "hF"), (pgn,))
                            pu, pun = mmbank()
                            mm_group(pu[:, :n], [(wu[wb][:, k, jj * 128:(jj + 1) * 128], hF[:, k, c0:c0 + n])
                                                 for k in range(KC)], (f"wu{wb}", "hF"), (pun,))
                            gb = cnt_[1] % 2
                            cnt_[1] += 1
                            act(gsb[gb][:, :n], pg[:, :n], AF.Silu, (pgn,), (f"gsb{gb}",))
                            dve_tt(aT[:, j, of:of + n], gsb[gb][:, :n], pu[:, :n], ALU.mult, (f"gsb{gb}", pun),
                                   (f"aT{j}",))
                for f in range(KC):
                    db = cnt_[2] % 2
                    cnt_[2] += 1
                    dma("pool", wd[db][:], w_down[l, f], (), (f"wd{db}",))
                    for (c0, n), of in zip(half, offs):
                        pb, pbn = mmbank()
                        mm_group(pb[:, :n], [(wd[db][:, j, :], aT[:, j, of:of + n]) for j in range(NJ)],
                                 (f"wd{db}",) + tuple(f"aT{j}" for j in range(NJ)), (pbn,))
                        resid_evac(pb, pbn, f, c0, n, l, 40)

        ffn(0, 0)
        dump("xb0", xr[:], [128, KC, NCOL], F32, tuple(f"xr{f}" for f in range(KC)))
        ckpt(8)
        mm_pool[0] = [0, 1, 2, 3, 6, 7]
        hi.release(m_hiB)
        lo.release(lo.lo)
        P.barrier(lambda e: e.memset(bscr[:], 0.0))
        xres_all = tuple(f"xr{f}" for f in range(KC))
        hF = lo.alloc("hF", [128, KC, NCOL], BF16)
        uT = hi.alloc("uT", [128, KC, NCOL], BF16)
        convst = hi.alloc("convst", [128, KC, CW - 1], F32)
        wdw_sb = hi.alloc("wdw_sb", [128, KC, CW], F32)
        zs_f = hi.alloc("zs_f", [128, KC, NS], F32)
        epsl = hi.alloc("epsl", [128, 1], F32)
        m_hiC = hi.mark()
        alloc_norm(512)
        for (c0, n) in pieces(0, 2176, 512):
            norm_prompt(xr[:, :, c0:c0 + n], xres_all, hF[:, :, c0:c0 + n], ("hF",), n, 2, 1, 0)
        norm_sample(xr[:, :, 2176:NCOL], xres_all, hF[:, :, 2176:NCOL], ("hF",), 2, 1, 0)
        hi.release(m_hiC)
        P.barrier(lambda e: e.memset(bscr[:], 0.0))
        ext_s = hi.alloc("ext_s", [128, KC, NS, CW], F32)
        wa = [hi.alloc(f"wa{i}", [128, KC, 128], BF16) for i in range(2)]
        wgt = [hi.alloc(f"wgt{i}", [128, KC, 128], BF16) for i in range(2)]
        gsb = [hi.alloc(f"gsb{i}", [128, 512], F32) for i in range(2)]
        ts1 = hi.alloc("ts1", [128, NS], F32)
        prodb = hi.alloc("prodb", [128, NS * CW], F32)
        P.add("pool", lambda e: e.memset(epsl[:], EPS), (), ("epsl",))
        dma("sp", wdw_sb[:], wdwT, (), ("wdw_sb",))
        dma("sp", ext_s[:, :, :, 0:CW - 1], stT, (), ("ext_s",))
        gc = 0
        for c in range(KC):
            wb = c % 2
            dma("pool", wa[wb][:], w_pw1[:, :, c * 128:(c + 1) * 128], (), (f"wa{wb}",))
            dma("pool", wgt[wb][:], w_pw1[:, :, D + c * 128:D + (c + 1) * 128], (), (f"wgt{wb}",))
            for (c0, n) in allp:
                pa, pan = mmbank()
                mm_group(pa[:, :n], [(wa[wb][:, k, :], hF[:, k, c0:c0 + n]) for k in range(KC)], (f"wa{wb}", "hF"), (pan,))
                pg, pgn = mmbank()
                mm_group(pg[:, :n], [(wgt[wb][:, k, :], hF[:, k, c0:c0 + n]) for k in range(KC)], (f"wgt{wb}", "hF"), (pgn,))
                gb = gc % 2
                gc += 1
                act(gsb[gb][:, :n], pg[:, :n], AF.Sigmoid, (pgn, "vec"), (f"gsb{gb}",),
                    bias=vec[:, V_BPW1 + 8 + c:V_BPW1 + 9 + c])
                ba = vec[:, V_BPW1 + c:V_BPW1 + c + 1]
                if c0 >= 2176:
                    dve_stt(ext_s[:, c, :, CW - 1], pa[:, :n], ba, gsb[gb][:, :n], ALU.add, ALU.mult,
                            (pan, "vec", f"gsb{gb}"), ("ext_s",))
                else:
                    if c0 + n == 2176:
                        dve_stt(convst[:, c, :], pa[:, n - 30:n], ba, gsb[gb][:, n - 30:n], ALU.add, ALU.mult,
                                (pan, "vec", f"gsb{gb}"), ("convst",))
                    dve_stt(uT[:, c, c0:c0 + n], pa[:, :n], ba, gsb[gb][:, :n], ALU.add, ALU.mult,
                            (pan, "vec", f"gsb{gb}"), (f"uT{c}",))
        dma("sp", convT, convst[:], ("convst",), ())
        dma("sp", convsT, ext_s[:, :, :, 1:CW], ("ext_s",), ())
        ures = tuple(f"uT{c}" for c in range(KC))
        dve_ts(uT[:, :, 0:128], uT[:, :, 0:128], flg[:, 0:1], None, ALU.mult, None, ures + ("flg",), ures)
        prodv = prodb[:].rearrange("p (b k) -> p b k", k=CW)
        for c in range(KC):
            dve_tt(prodv, ext_s[:, c, :, :], wdw_sb[:, c, :].unsqueeze(1).broadcast_to([128, NS, CW]), ALU.mult,
                   ("ext_s", "wdw_sb"), ("prodb",))
            P.add("dve", lambda e, o=ts1[:, :NS], i=prodv: e.tensor_reduce(o, i, AX.X, ALU.add), ("prodb",), ("ts1",))
            dve_ts(zs_f[:, c, :], ts1[:, :NS], vec[:, V_BDW + c:V_BDW + c + 1], None, ALU.add, None,
                   ("ts1", "vec"), ("zs_f",))
        hi.release(m_hiC)
        lo.release(lo.lo)
        P.barrier(lambda e: e.memset(bscr[:], 0.0))
        wp2 = lo.alloc("wp2", [128, KC, D], BF16)
        dg = [lo.alloc(f"dg{i}", [128, CW, 128], BF16) for i in range(2)]
        zf = hi.alloc("zf", [128, KC, 1024], F32)
        zb = [hi.alloc(f"zb{i}", [128, 512], BF16) for i in range(2)]
        zq = [hi.alloc(f"zq{i}", [128, 512], BF16) for i in range(2)]
        mean = hi.alloc("mean", [128, 512], F32)
        rln = hi.alloc("rln", [128, 512], F32)
        t1 = [hi.alloc(f"t1{i}", [128, 512], F32) for i in range(2)]
        sT = hi.alloc("sT", [128, KC, 512], BF16)
        dma("pool", wp2[:], w_pw2, (), ("wp2",))
        cnt2 = [0, 0, 0]

        def ln_pw2(zsrc, zres, c0, n):
            for c in range(KC):
                zi = cnt2[0] % 2
                cnt2[0] += 1
                P.add("dve", lambda e, o=zb[zi][:, :n], i=zsrc(c): e.tensor_copy(o, i), zres, (f"zb{zi}",))
                act(zq[zi][:, :n], zsrc(c), AF.Square, zres, (f"zq{zi}",))
                P.add("pe", lambda pe, o=ps_s[0][:, :n], r_=zb[zi][:, :n], c=c:
                      pe.matmul(o, ones_b[:], r_, start=(c == 0), stop=(c == KC - 1)), (f"zb{zi}", "ones_b"), ("ps_b4",))
                P.add("pe", lambda pe, o=ps_s[1][:, :n], r_=zq[zi][:, :n], c=c:
                      pe.matmul(o, ones_b[:], r_, start=(c == 0), stop=(c == KC - 1)), (f"zq{zi}", "ones_b"), ("ps_b5",))
            dve_ts(mean[:, :n], ps_s[0][:, :n], 1.0 / D, None, ALU.mult, None, ("ps_b4",), ("mean",))
            dve_tt(t1[0][:, :n], mean[:, :n], mean[:, :n], ALU.mult, ("mean",), ("t10",))
            dve_stt(rln[:, :n], ps_s[1][:, :n], 1.0 / D, t1[0][:, :n], ALU.mult, ALU.subtract, ("ps_b5", "t10"), ("rln",))
            act(rln[:, :n], rln[:, :n], AF.Ln, ("rln", "epsl"), ("rln",), bias=epsl[:, 0:1])
            act(rln[:, :n], rln[:, :n], AF.Exp, ("rln",), ("rln",), scale=-0.5)
            for c in range(KC):
                ti = cnt2[1] % 2
                cnt2[1] += 1
                dve_tt(t1[ti][:, :n], zsrc(c), mean[:, :n], ALU.subtract, zres + ("mean",), (f"t1{ti}",))
                dve_tt(t1[ti][:, :n], t1[ti][:, :n], rln[:, :n], ALU.mult, (f"t1{ti}", "rln"), (f"t1{ti}",), eng="pool")
                act(sT[:, c, :n], t1[ti][:, :n], AF.Silu, (f"t1{ti}", "vec"), ("sT",),
                    bias=vec[:, V_LNB + c:V_LNB + c + 1], scale=vec[:, V_LNG + c:V_LNG + c + 1])
            for f in range(KC):
                pb, pbn = mmbank()
                mm_group(pb[:, :n], [(wp2[:, k, f * 128:(f + 1) * 128], sT[:, k, :n]) for k in range(KC)],
                         ("wp2", "sT"), (pbn,))
                resid_evac(pb, pbn, f, c0, n, 1, 16, extra_bias=vec[:, V_BPW2 + f:V_BPW2 + f + 1])

        for half in range(2):
            hb = 128 + half * 1024
            for c in range(KC):
                db = cnt2[2] % 2
                cnt2[2] += 1
                for k in range(CW):
                    act(dg[db][:, k, :], ident[:], AF.Copy, ("ident", "wdw_sb"), (f"dg{db}",),
                        scale=wdw_sb[:, c, k:k + 1])
                for pc in range(2):
                    c0 = hb + pc * 512
                    pz, pzn = mmbank()
                    mm_group(pz[:, :512], [(dg[db][:, k, :], uT[:, c, c0 - 30 + k:c0 - 30 + k + 512]) for k in range(CW)],
                             (f"dg{db}",) + ures, (pzn,))
                    dve_ts(zf[:, c, pc * 512:(pc + 1) * 512], pz[:, :512], vec[:, V_BDW + c:V_BDW + c + 1], None,
                           ALU.add, None, (pzn, "vec"), (f"zf{pc}",))
            for pc in range(2):
                ln_pw2(lambda c, pc=pc: zf[:, c, pc * 512:(pc + 1) * 512], (f"zf{pc}",), hb + pc * 512, 512)
        ln_pw2(lambda c: zs_f[:, c, :], ("zs_f",), 2176, NS)
        mm_pool[0] = [0, 1, 2, 3, 4, 5, 6, 7]
        ffn(1, 128)
        dump("xb1", xr[:], [128, KC, NCOL], F32, tuple(f"xr{f}" for f in range(KC)))
        ckpt(9)
        hi.release(m_hiB)
        lo.release(lo.lo)
        P.barrier(lambda e: e.memset(bscr[:], 0.0))
        alloc_norm(512)
        yst = [hi.alloc(f"yst{i}", [128, KC, 512], F32) for i in range(2)]
        afin = hi.alloc("afin", [128, KC], F32)
        dve_ts(afin[:], vec[:, V_GFIN:V_GFIN + 8], 32.0, None, ALU.mult, None, ("vec",), ("afin",))
        xres_all = tuple(f"xr{f}" for f in range(KC))
        yc = 0
        for (c0, n) in pieces(128, 2176, 512) + [(2176, NS)]:
            yb = yc % 2
            yc += 1

            def fin(tmp, tn, yb=yb, c0=c0, n=n):
                dve_tt(yst[yb][:, :, :n], tmp[:, :, :n], afin[:].unsqueeze(2).broadcast_to([128, KC, n]), ALU.mult,
                       (tn, "afin"), (f"yst{yb}",))
                if c0 >= 2176:
                    dma("sp", ysT, yst[yb][:, :, :n], (f"yst{yb}",), ())
                else:
                    dma("sp", yT[:, :, c0 - 128:c0 - 128 + n], yst[yb][:, :, :n], (f"yst{yb}",), ())
            norm_stage(xr[:, :, c0:c0 + n], xres_all, n, "pool", fin)
        norm_flush()

    except _Stop:
        pass
    import contextlib
    with contextlib.ExitStack() as st:
        cnt = {e: st.enter_context(nc.semaphore(f"c_{e}")) for e in Prog.ENGS}
        pools = {"sp": [st.enter_context(nc.semaphore(f"d_sp{i}")) for i in range(24)],
                 "pool": [st.enter_context(nc.semaphore(f"d_pl{i}")) for i in range(8)],
                 "act": [st.enter_context(nc.semaphore(f"d_ac{i}")) for i in range(8)]}
        P.prepare(cnt, pools)
        block = st.enter_context(nc.Block())
        block.tensor(lambda e: P.emit_one("pe", e))
        block.scalar(lambda e: P.emit_one("act", e))
        block.vector(lambda e: P.emit_one("dve", e))
        block.gpsimd(lambda e: P.emit_one("pool", e))
        block.sync(lambda e: P.emit_one("sp", e))
    print("ops:", {e: len(P.ops[e]) for e in Prog.ENGS})
    return nc


def _t5_bucket(dist):
    dist = np.asarray(dist, np.int64)
    max_exact = 16
    n = np.maximum(dist, 1).astype(np.float32)
    large = max_exact + (np.log(n / max_exact) / np.float32(math.log(2048 / max_exact)) * (32 - max_exact)).astype(np.int32)
    large = np.minimum(large, 31)
    return np.where(dist < max_exact, dist, large)


NEG = np.float32(-30000.0)


def _pack_common(inp):
    f = np.float32
    vecs = np.zeros((128, 256), f)

    def put(col, v):
        v = np.asarray(v, f).reshape(-1, 128)
        vecs[:, col:col + v.shape[0]] = v.T
    for l in range(2):
        put(l * 48, inp["b_mod"][l])
        put(96 + l * 8, inp["g_mix"][l])
        put(112 + l * 8, inp["g_ffn"][l])
    put(128, inp["g_final"])
    put(136, inp["b_pw1"][0])
    put(152, inp["b_dw"][0])
    put(160, inp["ln_g"][0])
    put(168, inp["ln_b"][0])
    put(176, inp["b_pw2"][0])
    wq = np.asarray(inp["w_qkv"][0], f).reshape(8, 128, 3, 3, 8, 128)
    wqkv = np.ascontiguousarray(wq.transpose(2, 4, 1, 0, 3, 5)).reshape(24, 128, 8, 384)
    wd = np.asarray(inp["w_down"], f).reshape(2, NJ, 128, 8, 128)
    w_down = np.ascontiguousarray(wd.transpose(0, 3, 2, 1, 4))
    wdwT = np.ascontiguousarray(np.asarray(inp["w_dw"][0], f).reshape(CW, 8, 128).transpose(2, 1, 0))
    rb = np.asarray(inp["rel_bias"], f)
    eb = np.full((3, 8, 128, 3, 128), NEG, f)
    ki = np.arange(128)[:, None]
    qi = np.arange(128)[None, :]
    sb = np.zeros((128, 24), f)
    sb0 = np.zeros((128, 24), f)
    for g, (W, r) in enumerate(GROUPS):
        d_cur = qi - ki
        d_prev = qi + 128 - ki
        bc = _t5_bucket(np.clip(d_cur, 0, 128) * r)
        bp = _t5_bucket(np.clip(d_prev, 0, 128) * r)
        for h in range(8):
            col = g * 8 + h
            eb[g, h, :, 1, :] = np.where(d_cur >= 0, rb[bc, col], NEG)
            eb[g, h, :, 0, :] = np.where(d_prev <= 128, rb[bp, col], NEG)
            sb[:, col] = rb[_t5_bucket((128 - np.arange(128)) * r), col]
            sb0[:, col] = rb[0, col]
    eb[:, :, :, 2, :] = eb[:, :, :, 0, :]
    return dict(vecs=vecs, wqkv=wqkv, w_down=w_down, wdwT=wdwT, eb=eb.reshape(24, 128, 3, 128), sbias=sb, sbias0=sb0)


_NC_CACHE = {}


def make_in_maps(inp):
    f = np.float32
    com = _pack_common(inp)
    shared = dict(
        w_mod=np.asarray(inp["w_mod"], f), vecs=com["vecs"], wqkv=com["wqkv"],
        w_o=np.asarray(inp["w_o"][0], f), w_gate=np.asarray(inp["w_gate"], f), w_up=np.asarray(inp["w_up"], f),
        w_down=com["w_down"], w_pw1=np.asarray(inp["w_pw1"][0], f), w_pw2=np.asarray(inp["w_pw2"][0], f),
        wdwT=com["wdwT"], sbias=com["sbias"], sbias0=com["sbias0"], identd=np.eye(128, dtype=f))
    in_maps = []
    xp = np.asarray(inp["x_prompt"], f)
    for i in range(8):
        b, hf = i // 2, i % 2
        m = dict(shared)
        xT = np.zeros((D, SEQ), f)
        if hf == 1:
            xT[:, :] = xp[b].T
        else:
            xT[:, 2048:] = xp[b, :2048].T
        m["xT"] = xT
        sl = slice(NS * i, NS * (i + 1))
        m["xsT"] = np.ascontiguousarray(np.asarray(inp["x_sample"], f)[sl, 0, :].T)
        m["cT"] = np.ascontiguousarray(np.concatenate(
            [np.asarray(inp["c_prompt"], f)[b:b + 1], np.asarray(inp["c_sample"], f)[sl]], 0).T)
        m["flag"] = np.full((128, 1), float(hf), f)
        eb = com["eb"].copy()
        if hf == 0:
            eb[:, :, 2, :] = NEG
        m["ebias"] = eb
        for g, name in enumerate(("cache_kv_w128", "cache_kv_w512", "cache_kv_w2048")):
            c = np.asarray(inp[name], f)[0, sl]
            m[f"ck{g}"] = c.reshape(NS, 2, c.shape[2], D)
        st = np.asarray(inp["state_conv"], f)[0, sl]
        m["stT"] = np.ascontiguousarray(st.reshape(NS, CW - 1, 8, 128).transpose(3, 2, 0, 1))
        in_maps.append(m)
    return in_maps


def kernel(**inp):
    f = np.float32
    if "nc" not in _NC_CACHE:
        _NC_CACHE["nc"] = build()
    nc = _NC_CACHE["nc"]
    in_maps = make_in_maps(inp)
    res = run_bass_kernel_spmd(nc, in_maps, core_ids=list(range(8)))
    R = res.results
    _NC_CACHE["last"] = R
    y_prompt = np.zeros((4, SEQ, D), f)
    y_sample = np.zeros((128, 1, D), f)
    kvp = [np.zeros((1, 4, 2, W, 8, 128), f) for (W, r) in GROUPS]
    kvs = [np.zeros((1, 128, 2, 1, 8, 128), f) for _ in GROUPS]
    conv_p = np.zeros((1, 4, CW - 1, D), f)
    conv_s = np.zeros((1, 128, CW - 1, D), f)
    for i in range(8):
        b, hf = i // 2, i % 2
        sl = slice(NS * i, NS * (i + 1))
        r_ = R[i]
        y_prompt[b, hf * 2048:(hf + 1) * 2048, :] = r_["yT"].T
        y_sample[sl, 0, :] = r_["ysT"].T
        ks = r_["ksT"].reshape(128, 3, 8, NS)
        vs = r_["vsT"].reshape(128, 3, 8, NS)
        for g in range(3):
            kvs[g][0, sl, 0, 0] = ks[:, g].transpose(2, 1, 0)
            kvs[g][0, sl, 1, 0] = vs[:, g].transpose(2, 1, 0)
        conv_s[0, sl] = r_["convsT"].transpose(2, 3, 1, 0).reshape(NS, CW - 1, D)
        if hf == 1:
            for g, (W, r) in enumerate(GROUPS):
                kvp[g][0, b, 0] = r_[f"koutT{g}"].transpose(2, 0, 1)
                kvp[g][0, b, 1] = r_[f"vout{g}"].reshape(W, 8, 128)
            conv_p[0, b] = r_["convT"].transpose(2, 1, 0).reshape(CW - 1, D)
    return (y_prompt, y_sample, kvp[0], kvp[1], kvp[2], conv_p, kvs[0], kvs[1], kvs[2], conv_s)
```
